# Optimizing a Trainium2 kernel written in Bass

```python
import math
import jax, jax.numpy as jnp
from jax import lax
import numpy as np

D_MODEL = 1024
BATCH = 8
SEQ = 2048
DEPTH = 1

N_MLA_HEADS = 8
D_MLA_NOPE = 64
D_MLA_ROPE = 32
D_MLA_V = 64
Q_LORA = 256
KV_LORA = 128
N_DIFF_HEADS = 8
D_DIFF_HEAD = 64
D_FF = 4 * D_MODEL
ROPE_THETA = 10000.0
Q_BLOCK = 128
EPS = 1e-6

C_QA = Q_LORA
C_KVA = KV_LORA + D_MLA_ROPE
C_DQ = 2 * N_DIFF_HEADS * D_DIFF_HEAD
C_DK = 2 * N_DIFF_HEADS * D_DIFF_HEAD
C_DV = N_DIFF_HEADS * 2 * D_DIFF_HEAD
C_GATE = 2 * D_MODEL
D_IN_TOTAL = C_QA + C_KVA + C_DQ + C_DK + C_DV + C_GATE
D_MLA_OUT = N_MLA_HEADS * D_MLA_V
D_DIFF_OUT = N_DIFF_HEADS * 2 * D_DIFF_HEAD

kernel_name = "hybrid_mla_diffattn_gated_sqrelu"


def rmsnorm(x, g):
    xf = x.astype(jnp.float32)
    y = xf * lax.rsqrt(jnp.mean(xf * xf, axis=-1, keepdims=True) + EPS)
    return (y * g.astype(jnp.float32)).astype(x.dtype)


def rope(x, pos):
    d = x.shape[-1]
    inv_freq = 1.0 / (ROPE_THETA ** (jnp.arange(0, d, 2, dtype=jnp.float32) / d))
    ang = pos[:, None] * inv_freq[None, :]
    cos = jnp.cos(ang)[None, :, None, :]
    sin = jnp.sin(ang)[None, :, None, :]
    xf = x.astype(jnp.float32)
    x1, x2 = xf[..., : d // 2], xf[..., d // 2:]
    out = jnp.concatenate([x1 * cos - x2 * sin, x2 * cos + x1 * sin], axis=-1)
    return out.astype(x.dtype)


def causal_softmax(q_blk, k_pre, q_start, scale):
    s = jnp.einsum('bqhd,bkhd->bhqk', q_blk, k_pre, preferred_element_type=jnp.float32) * scale
    nq, nk = q_blk.shape[1], k_pre.shape[1]
    mask = (q_start + jnp.arange(nq))[:, None] >= jnp.arange(nk)[None, :]
    s = jnp.where(mask[None, None], s, -jnp.inf)
    return jax.nn.softmax(s, axis=-1)


def block_causal_attention(q, k, v, scale, mix_probs):
    S = q.shape[1]
    outs = []
    for start in range(0, S, Q_BLOCK):
        end = start + Q_BLOCK
        p = causal_softmax(q[:, start:end], k[:, :end], start, scale)
        w = mix_probs(p).astype(v.dtype)
        outs.append(jnp.einsum('bhqk,bkhd->bqhd', w, v[:, :end]))
    return jnp.concatenate(outs, axis=1)


def setup_inputs(seed: int = 0) -> dict:
    key = jax.random.key(seed)
    ks = jax.random.split(key, 24)

    def dense(k, fan_in, fan_out):
        return jax.random.normal(k, (DEPTH, fan_in, fan_out), jnp.float32) * fan_in ** -0.5

    def gain(k, d):
        return 1.0 + 0.01 * jax.random.normal(k, (DEPTH, d), jnp.float32)

    return {
        "x": jax.random.normal(ks[0], (BATCH, SEQ, D_MODEL), jnp.float32),
        "pre_attn_g": gain(ks[1], D_MODEL),
        "w_in": dense(ks[2], D_MODEL, D_IN_TOTAL),
        "q_norm_g": gain(ks[3], Q_LORA),
        "w_q_b": dense(ks[4], Q_LORA, N_MLA_HEADS * (D_MLA_NOPE + D_MLA_ROPE)),
        "kv_norm_g": gain(ks[5], KV_LORA),
        "w_kv_b": dense(ks[6], KV_LORA, N_MLA_HEADS * (D_MLA_NOPE + D_MLA_V)),
        "lambda_q1": 0.1 * jax.random.normal(ks[7], (DEPTH, D_DIFF_HEAD), jnp.float32),
        "lambda_k1": 0.1 * jax.random.normal(ks[8], (DEPTH, D_DIFF_HEAD), jnp.float32),
        "lambda_q2": 0.1 * jax.random.normal(ks[9], (DEPTH, D_DIFF_HEAD), jnp.float32),
        "lambda_k2": 0.1 * jax.random.normal(ks[10], (DEPTH, D_DIFF_HEAD), jnp.float32),
        "subln_g": gain(ks[11], 2 * D_DIFF_HEAD),
        "w_br_mla": dense(ks[12], D_MLA_OUT, D_MODEL),
        "w_br_diff": dense(ks[13], D_DIFF_OUT, D_MODEL),
        "w_out": dense(ks[14], D_MODEL, D_MODEL),
        "post_attn_g": gain(ks[15], D_MODEL),
        "pre_mlp_g": gain(ks[16], D_MODEL),
        "w_mlp_up": dense(ks[17], D_MODEL, D_FF),
        "w_mlp_down": dense(ks[18], D_FF, D_MODEL),
        "post_mlp_g": gain(ks[19], D_MODEL),
    }


def reference(x, pre_attn_g, w_in, q_norm_g, w_q_b, kv_norm_g, w_kv_b,
              lambda_q1, lambda_k1, lambda_q2, lambda_k2, subln_g,
              w_br_mla, w_br_diff, w_out, post_attn_g, pre_mlp_g,
              w_mlp_up, w_mlp_down, post_mlp_g):
    B, S, _ = x.shape
    pos = jnp.arange(S, dtype=jnp.float32)
    splits = [C_QA, C_QA + C_KVA, C_QA + C_KVA + C_DQ, C_QA + C_KVA + C_DQ + C_DK,
              C_QA + C_KVA + C_DQ + C_DK + C_DV]
    mla_scale = 1.0 / math.sqrt(D_MLA_NOPE + D_MLA_ROPE)
    diff_scale = 1.0 / math.sqrt(D_DIFF_HEAD)

    for l in range(DEPTH):
        h = rmsnorm(x, pre_attn_g[l])
        proj = jnp.einsum('bsd,de->bse', h, w_in[l])
        qa, kva, dq, dk, dv, gate_logits = jnp.split(proj, splits, axis=-1)

        q = jnp.einsum('bsr,re->bse', rmsnorm(qa, q_norm_g[l]), w_q_b[l])
        q = q.reshape(B, S, N_MLA_HEADS, D_MLA_NOPE + D_MLA_ROPE)
        q_mla = jnp.concatenate([q[..., :D_MLA_NOPE], rope(q[..., D_MLA_NOPE:], pos)], axis=-1)
        c_kv, k_rope = kva[..., :KV_LORA], kva[..., KV_LORA:]
        k_rope = rope(k_rope[:, :, None, :], pos)
        kv = jnp.einsum('bsr,re->bse', rmsnorm(c_kv, kv_norm_g[l]), w_kv_b[l])
        kv = kv.reshape(B, S, N_MLA_HEADS, D_MLA_NOPE + D_MLA_V)
        k_nope, v_mla = kv[..., :D_MLA_NOPE], kv[..., D_MLA_NOPE:]
        k_mla = jnp.concatenate(
            [k_nope, jnp.broadcast_to(k_rope, (B, S, N_MLA_HEADS, D_MLA_ROPE))], axis=-1)
        o_mla = block_causal_attention(q_mla, k_mla, v_mla, mla_scale, lambda p: p)
        u_mla = jnp.einsum('bse,ed->bsd', o_mla.reshape(B, S, D_MLA_OUT), w_br_mla[l])

        lam_init = 0.8 - 0.6 * math.exp(-0.3 * l)
        lam = (jnp.exp(jnp.sum(lambda_q1[l].astype(jnp.float32) * lambda_k1[l].astype(jnp.float32)))
               - jnp.exp(jnp.sum(lambda_q2[l].astype(jnp.float32) * lambda_k2[l].astype(jnp.float32)))
               + lam_init)
        q_d = rope(dq.reshape(B, S, 2 * N_DIFF_HEADS, D_DIFF_HEAD), pos)
        k_d = rope(dk.reshape(B, S, 2 * N_DIFF_HEADS, D_DIFF_HEAD), pos)
        v_d = dv.reshape(B, S, N_DIFF_HEADS, 2 * D_DIFF_HEAD)

        def diff_mix(p):
            p = p.reshape(B, N_DIFF_HEADS, 2, p.shape[2], p.shape[3])
            return p[:, :, 0] - lam * p[:, :, 1]

        o_d = block_causal_attention(q_d, k_d, v_d, diff_scale, diff_mix)
        o_d = rmsnorm(o_d, subln_g[l]) * (1.0 - lam_init)
        u_diff = jnp.einsum('bse,ed->bsd', o_d.reshape(B, S, D_DIFF_OUT), w_br_diff[l])

        g_mla, g_diff = jnp.split(gate_logits, 2, axis=-1)
        mixed = jax.nn.sigmoid(g_mla) * u_mla + jax.nn.sigmoid(g_diff) * u_diff
        y = jnp.einsum('bsd,de->bse', mixed, w_out[l])
        x = x + rmsnorm(y, post_attn_g[l])

        h = rmsnorm(x, pre_mlp_g[l])
        m = jnp.square(jax.nn.relu(jnp.einsum('bsd,df->bsf', h, w_mlp_up[l])))
        m = jnp.einsum('bsf,fd->bsd', m, w_mlp_down[l])
        x = x + rmsnorm(m, post_mlp_g[l])
    return x
```

```python
import math
from contextlib import ExitStack

import numpy as np
import concourse.bass as bass
import concourse.mybir as mybir
from concourse.bass_utils import run_bass_kernel_spmd

F32 = mybir.dt.float32
BF16 = mybir.dt.bfloat16
AF = mybir.ActivationFunctionType
ALU = mybir.AluOpType

D = 1024
S_LEN = 2048
NT = 16
NG = 4
EPS = 1e-6
DFF = 4096
MLA_SCALE = 1.0 / math.sqrt(96.0)
DIFF_SCALE = 1.0 / 8.0
LAM_INIT = 0.8 - 0.6 * math.exp(0.0)


def _caller_line():
    import sys
    f = sys._getframe(2)
    out = []
    while f is not None and len(out) < 3:
        out.append(f.f_lineno)
        f = f.f_back
    return out


class Buf:
    __slots__ = ("name", "w", "r", "sem")

    def __init__(self, name):
        self.name = name
        self.w = None
        self.r = {}
        self.sem = None


class Sched:
    def __init__(self, nc, stack):
        self.nc = nc
        self.stack = stack
        self.sems = {}
        self.semval = {}
        self.eng = {}
        for name in ("pe", "act", "dve", "pool", "sp"):
            s = self._newsem("e_" + name)
            self.eng[name] = dict(sem=s, ops=[], waited={})
        self.nbuf = 0
        self.labels = {}

    def _newsem(self, name):
        h = self.stack.enter_context(self.nc.semaphore(name))
        self.sems[name] = h
        self.semval[name] = 0
        return name

    def buf(self, name=None):
        self.nbuf += 1
        return Buf(f"{name or 'b'}_{self.nbuf}")

    def bufs(self, n, name="b"):
        return [self.buf(f"{name}{i}") for i in range(n)]

    def _deps(self, reads, writes):
        deps = {}

        def add(s, v):
            if deps.get(s, 0) < v:
                deps[s] = v

        for b in reads:
            if b.w is not None:
                add(*b.w)
        for b in writes:
            if b.w is not None:
                add(*b.w)
            for s, v in b.r.items():
                add(s, v)
        return deps

    def _waits(self, engname, deps):
        e = self.eng[engname]
        waits = []
        for s, v in deps.items():
            if engname == "pe" and s == e["sem"]:
                continue
            if e["waited"].get(s, 0) >= v:
                continue
            e["waited"][s] = v
            waits.append((s, v))
        return waits

    def _mark(self, tok, reads, writes):
        s, v = tok
        for b in reads:
            if b.r.get(s, 0) < v:
                b.r[s] = v
        for b in writes:
            b.w = tok
            b.r = {}

    def op(self, engname, fn, reads=(), writes=(), inc=True):
        e = self.eng[engname]
        waits = self._waits(engname, self._deps(reads, writes))
        s = e["sem"]
        if inc:
            self.semval[s] += 1
            tok = (s, self.semval[s])
        else:
            tok = (s, self.semval[s] + 1)
        e["ops"].append((waits, fn, (s, 1) if inc else None))
        self.labels.setdefault(engname, []).append((len(e["ops"]) - 1, tok, _caller_line()))
        self._mark(tok, reads, writes)
        return tok

    def dma(self, engname, fn, semb, reads=(), writes=()):
        e = self.eng[engname]
        if semb.sem is None:
            semb.sem = self._newsem("d_" + semb.name)
        waits = self._waits(engname, self._deps(reads, writes))
        s = semb.sem
        self.semval[s] += 16
        tok = (s, self.semval[s])
        e["ops"].append((waits, fn, (s, 16)))
        self._mark(tok, reads, writes)
        return tok

    def barrier(self):
        snap = dict(self.semval)
        for engname, e in self.eng.items():
            waits = []
            for s, v in snap.items():
                if v == 0 or (s == e["sem"] and engname == "pe"):
                    continue
                if e["waited"].get(s, 0) >= v:
                    continue
                e["waited"][s] = v
                waits.append((s, v))
            if waits:
                e["ops"].append((waits, None, None))

    def final_wait(self, engname):
        e = self.eng[engname]
        waits = [(s, v) for s, v in self.semval.items() if v > 0 and s != e["sem"]]
        e["ops"].append((waits, None, None))

    def emit(self):
        nc = self.nc
        sems = self.sems

        def run(engname):
            def body(eng):
                for waits, fn, inc in self.eng[engname]["ops"]:
                    for s, v in waits:
                        eng.wait_ge(sems[s], v)
                    if fn is None:
                        continue
                    inst = fn(eng)
                    if inc is not None:
                        inst.then_inc(sems[inc[0]], inc[1])
            return body

        with nc.Block() as block:
            block.sync(run("sp"))
            block.tensor(run("pe"))
            block.scalar(run("act"))
            block.vector(run("dve"))
            block.gpsimd(run("pool"))


class Arena:
    def __init__(self, ap, nbytes):
        self.ap = ap
        self.off = 0
        self.cap = nbytes
        self.peak = 0

    def alloc(self, shape, dt):
        esz = 4 if dt == F32 else 2
        nfree = 1
        for s in shape[1:]:
            nfree *= s
        nbytes = nfree * esz
        off = self.off
        self.off += (nbytes + 63) // 64 * 64
        self.peak = max(self.peak, self.off)
        assert self.off <= self.cap, f"SBUF arena overflow {self.off} > {self.cap}"
        v = self.ap[:, off // 2:(off + nbytes) // 2]
        if dt == F32:
            v = v.bitcast(F32)
        if len(shape) > 2:
            names = [f"d{i}" for i in range(len(shape) - 1)]
            kw = {n: s for n, s in zip(names[:-1], shape[1:-1])}
            v = v.rearrange(f"p ({' '.join(names)}) -> p {' '.join(names)}", **kw)
        if shape[0] < 128:
            v = v[0:shape[0]]
        return v

    def mark(self):
        return self.off

    def release(self, m):
        self.off = m


ARENA_BYTES = 207 * 1024


class _Stop(Exception):
    pass


def build_nc(dbg=None, stop=99):
    nc = bass.Bass("TRN2", target_bir_lowering=False)

    def din(name, shape):
        return nc.dram_tensor(name, list(shape), F32, kind="ExternalInput").ap()

    x = din("x", [S_LEN, D])
    gvec = din("gvec", [4, D])
    gq_d = din("gq", [128, 2])
    gkv_d = din("gkv", [128, 1])
    gsub_d = din("gsub", [1, 128])
    lamv_d = din("lamv", [1, 256])
    ident_d = din("ident", [128, 128])
    tri_d = din("tri", [128, 128])
    csd_d = din("csd", [128, 2, S_LEN])
    csm_d = din("csm", [128, 2, S_LEN])
    wm_d = din("wm", [128, 8, 448])
    wqb_d = din("wqb", [128, 2, 1536])
    wkvb_d = din("wkvb", [128, 1024])
    wd_d = din("wd", [8, 128, 8, 512])
    wdv_d = din("wdv", [128, 8, 1024])
    wb_d = din("wb", [8, 128, 28, 128])
    wout_d = din("wout", [128, 8, 1024])
    wup_d = din("wup", [16, 128, 8, 256])
    wdown_d = din("wdown", [128, 32, 1024])
    y = nc.dram_tensor("y", [S_LEN, D], F32, kind="ExternalOutput").ap()
    dbg_out = {}
    if dbg:
        for name, shape in dbg.items():
            dbg_out[name] = nc.dram_tensor("dbg_" + name, list(shape), F32, kind="ExternalOutput").ap()

    with ExitStack() as st:
        S = Sched(nc, st)
        arena_t = st.enter_context(nc.sbuf_tensor("arena", [128, ARENA_BYTES // 2], BF16))
        A = Arena(arena_t[:], ARENA_BYTES)
        banks = [st.enter_context(nc.psum_tensor(f"bank{i}", [128, 512], F32))[:] for i in range(8)]
        pb = S.bufs(8, "pb")
        bankT = [b.bitcast(BF16) for b in banks]

        def mm(out, lhsT, rhs, start, stop, reads, writes, inc, skip=False):
            if skip:
                S.op("pe", lambda e: e.matmul(out, lhsT=lhsT, rhs=rhs, start=start, stop=stop, skip_group_check=True),
                     reads, writes, inc)
            else:
                S.op("pe", lambda e: e.matmul(out, lhsT=lhsT, rhs=rhs, start=start, stop=stop), reads, writes, inc)

        def tr(out, in_, reads, writes, inc):
            S.op("pe", lambda e: e.transpose(out=out, in_=in_, identity=identb), reads + [b_const], writes, inc)

        def act(out, in_, func, reads, writes, scale=1.0, bias=0.0, accum=None):
            if accum is None:
                S.op("act", lambda e: e.activation(out=out, in_=in_, func=func, scale=scale, bias=bias), reads, writes)
            else:
                S.op("act", lambda e: e.activation(out=out, in_=in_, func=func, scale=scale, bias=bias, accum_out=accum),
                     reads, writes)

        def tt(eng, out, in0, in1, op, reads, writes):
            S.op(eng, lambda e: e.tensor_tensor(out=out, in0=in0, in1=in1, op=op), reads, writes)

        def ts(eng, out, in0, s1, reads, writes, op0=ALU.mult, s2=None, op1=None):
            if op1 is None:
                S.op(eng, lambda e: e.tensor_scalar(out=out, in0=in0, scalar1=s1, scalar2=None, op0=op0), reads, writes)
            else:
                S.op(eng, lambda e: e.tensor_scalar(out=out, in0=in0, scalar1=s1, scalar2=s2, op0=op0, op1=op1),
                     reads, writes)

        def stt(eng, out, in0, scalar, in1, op0, op1, reads, writes):
            S.op(eng, lambda e: e.scalar_tensor_tensor(out=out, in0=in0, scalar=scalar, in1=in1, op0=op0, op1=op1),
                 reads, writes)

        def recip(out, in_, reads, writes):
            S.op("dve", lambda e: e.reciprocal(out=out, in_=in_), reads, writes)

        def cp(eng, out, in_, reads, writes):
            if eng == "act":
                S.op("act", lambda e: e.copy(out=out, in_=in_), reads, writes)
            else:
                S.op(eng, lambda e: e.tensor_copy(out=out, in_=in_), reads, writes)

        def dma(eng, out, in_, semb, reads, writes):
            S.dma(eng, lambda e: e.dma_start(out=out, in_=in_), semb, reads, writes)

        def rstd_from(out, in_, n, reads_b, out_b):
            act(out, in_, AF.Sqrt, reads_b, [out_b], scale=1.0 / n, bias=EPS)
            recip(out, out, [out_b], [out_b])

        def dbg_dump(name, ap_f32, reads):
            if dbg and name in dbg_out:
                b = S.buf("dbg")
                dma("sp", dbg_out[name], ap_f32, b, reads, [b])

        identb = A.alloc([128, 128], BF16)
        trib = A.alloc([128, 128], BF16)
        onesf = A.alloc([128, 128], BF16)
        stage = A.alloc([128, 2, 128], F32)
        gb = A.alloc([128, 2, D], F32)
        xs = A.alloc([128, 2, D], F32)
        hb = A.alloc([128, 2, D], BF16)
        junk = A.alloc([128, D], BF16)
        stats = A.alloc([128, 256], F32)
        gq = A.alloc([128, 2], F32)
        gkv = A.alloc([128, 1], F32)
        gsub = A.alloc([128, 128], F32)
        lamv = A.alloc([128, 256], F32)
        b_const = S.buf("const")
        b_gb = S.bufs(2, "gb")
        b_xs = S.bufs(2, "xs")
        b_hb = S.bufs(2, "hb")
        stat_bufs = {}

        def sbuf_stat(c):
            if c not in stat_bufs:
                stat_bufs[c] = S.buf(f"stat{c}")
            return stat_bufs[c]

        dma("sp", stage[:, 0, :], ident_d[:, :], b_const, [], [b_const])
        dma("sp", stage[:, 1, :], tri_d[:, :], b_const, [], [b_const])
        dma("sp", gq, gq_d[:, :], b_const, [], [b_const])
        dma("sp", gkv, gkv_d[:, :], b_const, [], [b_const])
        dma("sp", gsub, gsub_d[0:1, :].partition_broadcast(128), b_const, [], [b_const])
        dma("sp", lamv, lamv_d[0:1, :].partition_broadcast(128), b_const, [], [b_const])
        dma("sp", gb[:, 0, :], gvec[0:1, :].partition_broadcast(128), b_gb[0], [], [b_gb[0]])
        dma("sp", gb[:, 1, :], gvec[1:2, :].partition_broadcast(128), b_gb[1], [], [b_gb[1]])
        b_c2 = S.buf("const2")
        cp("dve", identb, stage[:, 0, :], [b_const], [b_c2])
        cp("dve", trib, stage[:, 1, :], [b_const], [b_c2])
        S.op("pool", lambda e: e.memset(onesf, 1.0), [], [b_c2])
        S.op("pool", lambda e: e.memset(stats, 0.0), [], [b_c2])
        S.barrier()

        m_top = A.mark()
        r32 = A.alloc([128, 16384], BF16)
        mixedT = r32.rearrange("p (a b) -> p a b", a=8)
        b_mixed = [[S.buf(f"mx{c}_{n}") for n in range(NG)] for c in range(8)]
        m_s1 = A.mark()
        hT = A.alloc([128, 8, S_LEN], BF16)
        omlaT = A.alloc([128, 4, S_LEN], BF16)
        odT = A.alloc([128, 8, S_LEN], BF16)
        b_hT = S.bufs(NT, "hT")
        b_omlaT = [[S.buf(f"omT{p}_{n}") for n in range(NG)] for p in range(4)]
        b_odT = [[S.buf(f"odT{h}_{n}") for n in range(NG)] for h in range(8)]

        try:
            for t in range(NT):
                i = t % 2
                dma("sp", xs[:, i, :], x[t * 128:(t + 1) * 128, :], b_xs[i], [], [b_xs[i]])
                sc = stats[:, t:t + 1]
                bs = sbuf_stat(t)
                act(junk, xs[:, i, :], AF.Square, [b_xs[i]], [bs], accum=sc)
                rstd_from(sc, sc, D, [bs], bs)
                stt("dve", hb[:, i, :], xs[:, i, :], sc, gb[:, 0, :], ALU.mult, ALU.mult, [b_xs[i], bs, b_gb[0]], [b_hb[i]])
                pT = bankT[i].rearrange("p (a b) -> p a b", a=8)
                for c in range(8):
                    tr(pT[:, c, :], hb[:, i, c * 128:(c + 1) * 128], [b_hb[i]], [pb[i]], c == 7)
                if t >= 1:
                    j = (t - 1) % 2
                    cp("act", hT[:, :, (t - 1) * 128:t * 128], bankT[j].rearrange("p (a b) -> p a b", a=8),
                       [pb[j]], [b_hT[t - 1]])
            cp("act", hT[:, :, (NT - 1) * 128:NT * 128], bankT[(NT - 1) % 2].rearrange("p (a b) -> p a b", a=8),
               [pb[(NT - 1) % 2]], [b_hT[NT - 1]])
            if dbg and "hT" in dbg_out:
                m = A.mark()
                tmpf = A.alloc([128, 8, S_LEN // 4], F32)
                bt = S.buf("dbgt")
                cp("dve", tmpf, hT[:, :, 0:512], b_hT[0:4], [bt])
                dbg_dump("hT", tmpf, [bt])
                S.barrier()
                A.release(m)

            if stop <= 0:
                raise _Stop()
            S.barrier()
            m_mla = A.mark()
            R = Arena(r32, 32768)
            wm = A.alloc([128, 8, 448], BF16)
            wqb = A.alloc([128, 2, 1536], BF16)
            wkvb = A.alloc([128, 1024], BF16)
            csm = R.alloc([128, 2, S_LEN], F32)
            qag = R.alloc([128, 2, S_LEN], BF16)
            ckvg = A.alloc([128, S_LEN], BF16)
            rq = A.alloc([128, 512], F32)
            rkv = A.alloc([128, 512], F32)
            krT = A.alloc([32, S_LEN], BF16)
            Vm = A.alloc([128, NT, 8, 65], BF16)
            QT = R.alloc([128, 2, S_LEN], BF16)
            KT = A.alloc([128, 2, S_LEN], BF16)
            opair = A.alloc([128, 2, 4, 128], BF16)
            PT = A.alloc([128, 4, 512], BF16)
            sqf = A.alloc([128, 2, 512], F32)
            sqh = A.alloc([128, 2, 512], BF16)
            sql = A.alloc([128, 2, 512], BF16)
            t1 = xs[:, 0, :].rearrange("p (a b) -> p a b", a=2)
            t2 = xs[:, 1, :].rearrange("p (a b) -> p a b", a=2)
            b_wm, b_wqb, b_wkvb, b_csm = S.buf("wm"), S.buf("wqb"), S.buf("wkvb"), S.buf("csm")
            b_qag = S.bufs(NG, "qag")
            b_ckvg = S.bufs(NG, "ckvg")
            b_rq, b_rkv = S.buf("rq"), S.buf("rkv")
            b_krT = S.bufs(NG, "krT")
            b_Vm = S.bufs(NT, "Vm")
            b_Vm1 = S.buf("Vm1")
            b_QT = [S.bufs(NG, f"QT{hh}") for hh in range(2)]
            b_KT = [S.bufs(NG, f"KT{hh}") for hh in range(2)]
            b_KTr = S.bufs(2, "KTr")
            b_opair = S.bufs(2, "opair")
            b_PT = S.bufs(4, "PT")
            b_sqf = S.bufs(2, "sqf")
            b_sqh = S.bufs(2, "sqh")
            b_sql = S.bufs(2, "sql")
            b_t1 = S.bufs(2, "t1")
            b_t2 = S.bufs(2, "t2")

            dma("pool", wm, wm_d[:, :, :], b_wm, [], [b_wm])
            dma("pool", wqb, wqb_d[:, :, :], b_wqb, [], [b_wqb])
            dma("pool", wkvb, wkvb_d[:, :], b_wkvb, [], [b_wkvb])
            dma("sp", csm, csm_d[:, :, :], b_csm, [], [b_csm])
            S.op("pool", lambda e: e.memset(Vm[:, :, :, 64:65], 1.0), [], [b_Vm1])

            for n in range(NG):
                tok = slice(n * 512, (n + 1) * 512)
                hbufs = b_hT[4 * n:4 * n + 4]
                for m in range(2):
                    for kc in range(8):
                        mm(banks[m], wm[:, kc, m * 128:(m + 1) * 128], hT[:, kc, tok], kc == 0, kc == 7,
                           [b_wm] + hbufs, [pb[m]], kc == 7)
                    act(sqf[:, m, :], banks[m], AF.Square, [pb[m]], [b_sqf[m]])
                    cp("dve", sqh[:, m, :], sqf[:, m, :], [b_sqf[m]], [b_sqh[m]])
                    tt("dve", sql[:, m, :], sqf[:, m, :], sqh[:, m, :], ALU.subtract, [b_sqf[m], b_sqh[m]], [b_sql[m]])
                for m in range(2):
                    mm(banks[2], onesf, sqh[:, m, :], m == 0, False, [b_const, b_sqh[m]], [pb[2]], False)
                    mm(banks[2], onesf, sql[:, m, :], False, m == 1, [b_const, b_sql[m]], [pb[2]], m == 1)
                rstd_from(rq, banks[2], 256, [pb[2]], b_rq)
                for m in range(2):
                    stt("dve", qag[:, m, tok], banks[m], gq[:, m:m + 1], rq, ALU.mult, ALU.mult,
                        [pb[m], b_const, b_rq], [b_qag[n]])
                for kc in range(8):
                    mm(banks[3], wm[:, kc, 256:384], hT[:, kc, tok], kc == 0, kc == 7, [b_wm] + hbufs, [pb[3]], kc == 7)
                act(sqf[:, 0, :], banks[3], AF.Square, [pb[3]], [b_sqf[0]])
                cp("dve", sqh[:, 0, :], sqf[:, 0, :], [b_sqf[0]], [b_sqh[0]])
                tt("dve", sql[:, 0, :], sqf[:, 0, :], sqh[:, 0, :], ALU.subtract, [b_sqf[0], b_sqh[0]], [b_sql[0]])
                mm(banks[2], onesf, sqh[:, 0, :], True, False, [b_const, b_sqh[0]], [pb[2]], False)
                mm(banks[2], onesf, sql[:, 0, :], False, True, [b_const, b_sql[0]], [pb[2]], True)
                rstd_from(rkv, banks[2], 128, [pb[2]], b_rkv)
                stt("dve", ckvg[:, tok], banks[3], gkv[:, 0:1], rkv, ALU.mult, ALU.mult, [pb[3], b_const, b_rkv], [b_ckvg[n]])
                for kc in range(8):
                    mm(banks[4][0:32, :], wm[:, kc, 384:416], hT[:, kc, tok], kc == 0, kc == 7, [b_wm] + hbufs, [pb[4]], kc == 7)
                for kc in range(8):
                    mm(banks[5][0:32, :], wm[:, kc, 416:448], hT[:, kc, tok], kc == 0, kc == 7, [b_wm] + hbufs, [pb[5]], kc == 7)
                tt("dve", t1[0:32, 0, :], banks[4][0:32, :], csm[0:32, 0, tok], ALU.mult, [pb[4], b_csm], [b_t1[0]])
                tt("dve", t2[0:32, 0, :], banks[5][0:32, :], csm[0:32, 1, tok], ALU.mult, [pb[5], b_csm], [b_t2[0]])
                tt("pool", krT[0:32, tok], t1[0:32, 0, :], t2[0:32, 0, :], ALU.add, [b_t1[0], b_t2[0]], [b_krT[n]])

            for t in range(NT):
                bi = 6 + t % 2
                mm(banks[bi], ckvg[:, t * 128:(t + 1) * 128], wkvb[:, 512:1024], True, True,
                   [b_ckvg[t // 4], b_wkvb], [pb[bi]], True)
                cp("act", Vm[:, t, :, 0:64], banks[bi].rearrange("p (a b) -> p a b", a=8), [pb[bi]], [b_Vm[t]])

            if stop <= 1:
                raise _Stop()
            grp_cnt = 0
            for p in range(4):
                for hh in range(2):
                    h = 2 * p + hh
                    for n in range(NG):
                        tok = slice(n * 512, (n + 1) * 512)
                        ba, bb_, bk = (0, 1, 2) if n % 2 == 0 else (3, 4, 5)
                        for kc in range(2):
                            mm(banks[ba][0:96, :], wqb[:, kc, h * 96:(h + 1) * 96], qag[:, kc, tok], kc == 0, kc == 1,
                               [b_wqb, b_qag[n]], [pb[ba]], kc == 1)
                        for kc in range(2):
                            mm(banks[bb_][0:96, :], wqb[:, kc, 768 + h * 96:768 + (h + 1) * 96], qag[:, kc, tok], kc == 0, kc == 1,
                               [b_wqb, b_qag[n]], [pb[bb_]], kc == 1)
                        mm(banks[bk][0:64, :], wkvb[:, h * 64:(h + 1) * 64], ckvg[:, tok], True, True,
                           [b_wkvb, b_ckvg[n]], [pb[bk]], True)
                        k = n % 2
                        cp("act", QT[0:64, hh, tok], banks[ba][0:64, :], [pb[ba]], [b_QT[hh][n]])
                        tt("dve", t1[64:96, k, :], banks[ba][64:96, :], csm[64:96, 0, tok], ALU.mult, [pb[ba], b_csm], [b_t1[k]])
                        tt("dve", t2[64:96, k, :], banks[bb_][64:96, :], csm[64:96, 1, tok], ALU.mult, [pb[bb_], b_csm], [b_t2[k]])
                        tt("pool", QT[64:96, hh, tok], t1[64:96, k, :], t2[64:96, k, :], ALU.add, [b_t1[k], b_t2[k]], [b_QT[hh][n]])
                        cp("act", KT[0:64, hh, tok], banks[bk][0:64, :], [pb[bk]], [b_KT[hh][n]])
                    dma("sp", KT[64:96, hh, :], krT[0:32, :], b_KTr[hh], b_krT, [b_KTr[hh]])
                for I in range(NG):
                    osl = grp_cnt % 2
                    for hh in range(2):
                        h = 2 * p + hh
                        ob = 6 + (grp_cnt * 2 + hh) % 2
                        acc = banks[ob][:, 0:260].rearrange("p (a b) -> p a b", a=4)
                        nj = 4 * I + 4
                        qend = (I + 1) * 512
                        state = {"first": True}

                        def qk(j):
                            q0 = max(I * 512, j * 128)
                            N = qend - q0
                            sbk = j % 2
                            pslot = j % 2 + 2 * hh
                            krd = [b_KT[hh][j // 4], b_KTr[hh]]
                            qrd = [b_QT[hh][I]]
                            mm(banks[sbk][:, 0:N], KT[0:96, hh, j * 128:(j + 1) * 128], QT[0:96, hh, q0:qend], True, True,
                               krd + qrd, [pb[sbk]], True)
                            act(PT[:, pslot, 0:N], banks[sbk][:, 0:N], AF.Exp, [pb[sbk]], [b_PT[pslot]], scale=MLA_SCALE)
                            if j >= 4 * I:
                                tt("pool", PT[:, pslot, 0:128], PT[:, pslot, 0:128], trib, ALU.mult,
                                   [b_PT[pslot], b_const], [b_PT[pslot]])

                        def pv(j):
                            q0 = max(I * 512, j * 128)
                            pslot = j % 2 + 2 * hh
                            ilist = list(range(max(4 * I, j), 4 * I + 4))
                            for i in ilist:
                                c0 = i * 128 - q0
                                mm(acc[:, i - 4 * I, :], PT[:, pslot, c0:c0 + 128], Vm[:, j, h, :], state["first"], i == j,
                                   [b_PT[pslot], b_Vm[j], b_Vm1], [pb[ob]], i == ilist[-1], skip=True)
                                state["first"] = False

                        qk(0)
                        for j in range(nj):
                            if j + 1 < nj:
                                qk(j + 1)
                            pv(j)
                        rc = stats[:, 32 + 4 * hh:36 + 4 * hh]
                        brc = sbuf_stat(32 + hh)
                        recip(rc, acc[:, :, 64], [pb[ob]], [brc])
                        for il in range(4):
                            ts("dve", opair[:, osl, il, hh * 64:(hh + 1) * 64], acc[:, il, 0:64], rc[:, il:il + 1],
                               [pb[ob], brc], [b_opair[osl]])
                    tb = 2
                    for il in range(4):
                        tr(bankT[tb][:, il * 128:(il + 1) * 128], opair[:, osl, il, :], [b_opair[osl]], [pb[tb]], il == 3)
                    cp("act", omlaT[:, p, I * 512:(I + 1) * 512], bankT[tb][:, 0:512], [pb[tb]], [b_omlaT[p][I]])
                    grp_cnt += 1
            S.barrier()
            A.release(m_mla)
            if dbg and "omlaT" in dbg_out:
                m = A.mark()
                tmpf = A.alloc([128, 4, 512], F32)
                bt = S.buf("dbgt")
                cp("dve", tmpf, omlaT[:, :, 0:512], [b_omlaT[p_][0] for p_ in range(4)], [bt])
                dbg_dump("omlaT", tmpf, [bt])
                S.barrier()
                A.release(m)

            if stop <= 2:
                raise _Stop()
            m_diff = A.mark()
            R = Arena(r32, 32768)
            csd = R.alloc([128, 2, S_LEN], F32)
            wdv = R.alloc([128, 8, 1024], BF16)
            Vd = A.alloc([128, NT, 8, 129], BF16)
            wdt = A.alloc([128, 2, 8, 512], BF16)
            QdT = hb.rearrange("p a b -> p (a b)")
            KdT = A.alloc([128, S_LEN], BF16)
            PTd = A.alloc([128, 4, 512], BF16)
            osb = A.alloc([128, 1, 8, 129], F32)
            of32 = A.alloc([128, 2, 128], F32)
            tA = A.alloc([128, 2, 128], F32)
            odt = A.alloc([128, 2, 128], BF16)
            b_csd, b_wdv = S.buf("csd"), S.buf("wdv")
            b_Vd = S.bufs(NT, "Vd")
            b_Vd1 = S.buf("Vd1")
            b_wdt = S.bufs(2, "wdt")
            b_QdT = S.bufs(NG, "QdT")
            b_KdT = S.bufs(NG, "KdT")
            b_PTd = S.bufs(4, "PTd")
            b_t1 = S.bufs(2, "t1d")
            b_t2 = S.bufs(2, "t2d")
            b_osb = S.bufs(1, "osb")
            b_of = S.bufs(2, "of32")
            b_tA = S.bufs(2, "tA")
            b_odt = S.bufs(2, "odt")
            b_lam = S.buf("lam")

            dma("sp", csd, csd_d[:, :, :], b_csd, [], [b_csd])
            dma("pool", wdv, wdv_d[:, :, :], b_wdv, [], [b_wdv])
            dma("pool", wdt[:, 0], wd_d[0], b_wdt[0], [], [b_wdt[0]])
            S.op("pool", lambda e: e.memset(Vd[:, :, :, 128:129], 1.0), [], [b_Vd1])

            lcol = stats[:, 48:56]
            tt("dve", of32[:, 0, 0:64], lamv[:, 0:64], lamv[:, 64:128], ALU.mult, [b_const], [b_of[0]])
            tt("dve", of32[:, 0, 64:128], lamv[:, 128:192], lamv[:, 192:256], ALU.mult, [b_const], [b_of[0]])
            S.op("dve", lambda e: e.reduce_sum(out=lcol[:, 0:2], in_=of32[:, 0, :].rearrange("p (a b) -> p a b", a=2),
                                               axis=mybir.AxisListType.X), [b_of[0]], [b_lam])
            act(lcol[:, 2:4], lcol[:, 0:2], AF.Exp, [b_lam], [b_lam])
            tt("dve", lcol[:, 4:5], lcol[:, 3:4], lcol[:, 2:3], ALU.subtract, [b_lam], [b_lam])
            ts("dve", lcol[:, 5:6], lcol[:, 4:5], -LAM_INIT, [b_lam], [b_lam], op0=ALU.add)
            ts("dve", gsub, gsub, 1.0 - LAM_INIT, [b_const, b_lam], [b_lam])
            neglam = lcol[:, 5:6]

            for t in range(NT):
                for g in range(2):
                    bi = (2 * t + g) % 4
                    for kc in range(8):
                        mm(banks[bi], hT[:, kc, t * 128:(t + 1) * 128], wdv[:, kc, g * 512:(g + 1) * 512], kc == 0, kc == 7,
                           [b_hT[t], b_wdv], [pb[bi]], kc == 7)
                    cp("act" if g == 0 else "dve", Vd[:, t, 4 * g:4 * g + 4, 0:128],
                       banks[bi].rearrange("p (a b) -> p a b", a=4), [pb[bi]], [b_Vd[t]])

            gcnt = 0
            for h in range(8):
                ws = h % 2
                if h + 1 < 8:
                    dma("pool", wdt[:, (h + 1) % 2], wd_d[h + 1], b_wdt[(h + 1) % 2], [], [b_wdt[(h + 1) % 2]])
                for n in range(NG):
                    tok = slice(n * 512, (n + 1) * 512)
                    hbufs = b_hT[4 * n:4 * n + 4]
                    for qk_ in range(2):
                        k = qk_
                        ba, bb_ = (0, 1) if qk_ == 0 else (2, 3)
                        for kc in range(8):
                            mm(banks[ba], wdt[:, ws, kc, qk_ * 256:qk_ * 256 + 128], hT[:, kc, tok], kc == 0, kc == 7,
                               [b_wdt[ws]] + hbufs, [pb[ba]], kc == 7)
                        for kc in range(8):
                            mm(banks[bb_], wdt[:, ws, kc, qk_ * 256 + 128:qk_ * 256 + 256], hT[:, kc, tok], kc == 0, kc == 7,
                               [b_wdt[ws]] + hbufs, [pb[bb_]], kc == 7)
                        dst, bdst = (QdT, b_QdT) if qk_ == 0 else (KdT, b_KdT)
                        tt("dve", t1[:, k, :], banks[ba], csd[:, 0, tok], ALU.mult, [pb[ba], b_csd], [b_t1[k]])
                        tt("dve", t2[:, k, :], banks[bb_], csd[:, 1, tok], ALU.mult, [pb[bb_], b_csd], [b_t2[k]])
                        tt("pool", dst[:, tok], t1[:, k, :], t2[:, k, :], ALU.add, [b_t1[k], b_t2[k]], [bdst[n]])
                for I in range(NG):
                    nj = 4 * I + 4
                    qend = (I + 1) * 512
                    touched = set()

                    def accv(m, il):
                        a = m * 4 + il
                        bk = 4 + a // 3
                        c = (a % 3) * 129
                        return bk, banks[bk][:, c:c + 129]

                    def qk(j):
                        q0 = max(I * 512, j * 128)
                        N = qend - q0
                        for m in range(2):
                            sbk = 2 * (j % 2) + m
                            pslot = 2 * (j % 2) + m
                            mm(banks[sbk][:, 0:N], KdT[m * 64:(m + 1) * 64, j * 128:(j + 1) * 128],
                               QdT[m * 64:(m + 1) * 64, q0:qend], True, True, [b_KdT[j // 4], b_QdT[I]], [pb[sbk]], True)
                            act(PTd[:, pslot, 0:N], banks[sbk][:, 0:N], AF.Exp, [pb[sbk]], [b_PTd[pslot]], scale=DIFF_SCALE)
                            if j >= 4 * I:
                                tt("pool", PTd[:, pslot, 0:128], PTd[:, pslot, 0:128], trib, ALU.mult,
                                   [b_PTd[pslot], b_const], [b_PTd[pslot]])

                    def pv(j):
                        q0 = max(I * 512, j * 128)
                        ilist = list(range(max(4 * I, j), 4 * I + 4))
                        for m in range(2):
                            pslot = 2 * (j % 2) + m
                            for i in ilist:
                                c0 = i * 128 - q0
                                bk, av = accv(m, i - 4 * I)
                                mm(av, PTd[:, pslot, c0:c0 + 128], Vd[:, j, h, :], bk not in touched, i == j,
                                   [b_PTd[pslot], b_Vd[j], b_Vd1], [pb[bk]], (m == 1 and i == ilist[-1]), skip=True)
                                touched.add(bk)

                    qk(0)
                    for j in range(nj):
                        if j + 1 < nj:
                            qk(j + 1)
                        pv(j)
                    os_ = 0
                    osv = osb[:, os_].rearrange("p a b -> p (a b)")
                    cp("dve", osv[:, 0:387], banks[4][:, 0:387], [pb[4]], [b_osb[os_]])
                    cp("act", osv[:, 387:774], banks[5][:, 0:387], [pb[5]], [b_osb[os_]])
                    cp("dve", osv[:, 774:1032], banks[6][:, 0:258], [pb[6]], [b_osb[os_]])
                    rcs = stats[:, 64 + 8 * os_:72 + 8 * os_]
                    brc = sbuf_stat(64 + os_)
                    recip(rcs, osb[:, os_, :, 128], [b_osb[os_]], [brc])
                    ts("dve", rcs[:, 4:8], rcs[:, 4:8], neglam, [brc, b_lam], [brc])
                    for il in range(4):
                        k = il % 2
                        ts("dve", tA[:, k, :], osb[:, os_, 4 + il, 0:128], rcs[:, 4 + il:5 + il], [b_osb[os_], brc], [b_tA[k]])
                        stt("dve", of32[:, k, :], osb[:, os_, il, 0:128], rcs[:, il:il + 1], tA[:, k, :], ALU.mult, ALU.add,
                            [b_osb[os_], brc, b_tA[k]], [b_of[k]])
                        sc = stats[:, 80 + 4 * os_ + il:81 + 4 * os_ + il]
                        bs = sbuf_stat(80 + 4 * os_ + il)
                        act(junk[:, 0:128], of32[:, k, :], AF.Square, [b_of[k]], [bs], accum=sc)
                        rstd_from(sc, sc, 128, [bs], bs)
                        stt("dve", odt[:, k, :], of32[:, k, :], sc, gsub, ALU.mult, ALU.mult, [b_of[k], bs, b_lam], [b_odt[k]])
                        tr(bankT[7][:, il * 128:(il + 1) * 128], odt[:, k, :], [b_odt[k]], [pb[7]], True)
                    cp("act", odT[:, h, I * 512:(I + 1) * 512], bankT[7][:, 0:512], [pb[7]], [b_odT[h][I]])
                    gcnt += 1
            S.barrier()
            A.release(m_diff)
            if dbg and "odT" in dbg_out:
                m = A.mark()
                tmpf = A.alloc([128, 8, 512], F32)
                bt = S.buf("dbgt")
                cp("dve", tmpf, odT[:, :, 0:512], [b_odT[h_][0] for h_ in range(8)], [bt])
                dbg_dump("odT", tmpf, [bt])
                S.barrier()
                A.release(m)

            if stop <= 3:
                raise _Stop()
            m_b1 = A.mark()
            wbt = A.alloc([128, 3, 28, 128], BF16)
            s1 = A.alloc([128, 2, 512], F32)
            s2 = A.alloc([128, 2, 512], F32)
            m1 = A.alloc([128, 2, 512], F32)
            m2 = A.alloc([128, 2, 512], F32)
            b_wbt = S.bufs(3, "wbt")
            b_s1, b_s2, b_m1, b_m2 = S.bufs(2, "s1"), S.bufs(2, "s2"), S.bufs(2, "m1"), S.bufs(2, "m2")
            for c in range(2):
                dma("pool", wbt[:, c], wb_d[c], b_wbt[c], [], [b_wbt[c]])
            cnt = 0
            for c in range(8):
                if c + 2 < 8:
                    dma("pool", wbt[:, (c + 2) % 3], wb_d[c + 2], b_wbt[(c + 2) % 3], [], [b_wbt[(c + 2) % 3]])
                w = wbt[:, c % 3]
                bw = b_wbt[c % 3]
                for n in range(NG):
                    tok = slice(n * 512, (n + 1) * 512)
                    k = cnt % 2
                    bU1, bU2, bG1, bG2 = [4 * k + i for i in range(4)]
                    for kc in range(4):
                        mm(banks[bU1], w[:, kc, :], omlaT[:, kc, tok], kc == 0, kc == 3, [bw, b_omlaT[kc][n]], [pb[bU1]], kc == 3)
                    for kc in range(8):
                        mm(banks[bU2], w[:, 4 + kc, :], odT[:, kc, tok], kc == 0, kc == 7, [bw, b_odT[kc][n]], [pb[bU2]], kc == 7)
                    hbufs = b_hT[4 * n:4 * n + 4]
                    for kc in range(8):
                        mm(banks[bG1], w[:, 12 + kc, :], hT[:, kc, tok], kc == 0, kc == 7, [bw] + hbufs, [pb[bG1]], kc == 7)
                    for kc in range(8):
                        mm(banks[bG2], w[:, 20 + kc, :], hT[:, kc, tok], kc == 0, kc == 7, [bw] + hbufs, [pb[bG2]], kc == 7)
                    act(s1[:, k, :], banks[bG1], AF.Sigmoid, [pb[bG1]], [b_s1[k]])
                    act(s2[:, k, :], banks[bG2], AF.Sigmoid, [pb[bG2]], [b_s2[k]])
                    tt("dve", m1[:, k, :], banks[bU1], s1[:, k, :], ALU.mult, [pb[bU1], b_s1[k]], [b_m1[k]])
                    tt("dve", m2[:, k, :], banks[bU2], s2[:, k, :], ALU.mult, [pb[bU2], b_s2[k]], [b_m2[k]])
                    tt("pool", mixedT[:, c, tok], m1[:, k, :], m2[:, k, :], ALU.add, [b_m1[k], b_m2[k]], [b_mixed[c][n]])
                    cnt += 1
            if dbg and "mixedT" in dbg_out:
                m = A.mark()
                tmpf = A.alloc([128, 8, 512], F32)
                bt = S.buf("dbgt")
                cp("dve", tmpf, mixedT[:, :, 0:512], [b_mixed[c_][0] for c_ in range(8)], [bt])
                dbg_dump("mixedT", tmpf, [bt])
                S.barrier()
                A.release(m)
            S.barrier()
            A.release(m_s1)

            if stop <= 4:
                raise _Stop()
            x1 = A.alloc([128, NT, D], F32)
            wdown = A.alloc([128, 32, 1024], BF16)
            b_x1 = S.bufs(NT, "x1")
            b_wdown = S.bufs(4, "wdown")
            m_b2 = A.mark()
            wout = wdown[:, 24:32, :]
            yn = A.alloc([128, 2, D], F32)
            b_wout = S.buf("wout")
            b_yn = S.bufs(2, "yn")
            dma("pool", wout, wout_d[:, :, :], b_wout, [], [b_wout])
            for q in range(3):
                dma("pool", wdown[:, 8 * q:8 * q + 8, :], wdown_d[:, 8 * q:8 * q + 8, :], b_wdown[q], [], [b_wdown[q]])
            for t in range(NT):
                i = t % 2
                dma("sp", xs[:, i, :], x[t * 128:(t + 1) * 128, :], b_xs[i], [], [b_xs[i]])
                tsl = slice(t * 128, (t + 1) * 128)
                b0 = 4 * i
                mrd = [b_mixed[c_][t // 4] for c_ in range(8)]
                for half in range(2):
                    for kc in range(8):
                        mm(banks[b0 + half], mixedT[:, kc, tsl], wout[:, kc, half * 512:(half + 1) * 512], kc == 0, kc == 7,
                           mrd + [b_wout], [pb[b0 + half]], kc == 7)
                sc2 = stats[:, 96 + 2 * i:98 + 2 * i]
                bs = sbuf_stat(96 + i)
                for half in range(2):
                    act(junk[:, 0:512], banks[b0 + half], AF.Square, [pb[b0 + half]], [bs], accum=sc2[:, half:half + 1])
                tt("dve", sc2[:, 0:1], sc2[:, 0:1], sc2[:, 1:2], ALU.add, [bs], [bs])
                rstd_from(sc2[:, 0:1], sc2[:, 0:1], D, [bs], bs)
                for half in range(2):
                    hs = slice(half * 512, (half + 1) * 512)
                    stt("dve", yn[:, i, hs], banks[b0 + half], sc2[:, 0:1], gb[:, 1, hs], ALU.mult, ALU.mult,
                        [pb[b0 + half], bs, b_gb[1]], [b_yn[i]])
                tt("pool", x1[:, t, :], yn[:, i, :], xs[:, i, :], ALU.add, [b_yn[i], b_xs[i]], [b_x1[t]])
            if dbg and "x1" in dbg_out:
                bt = S.buf("dbgt")
                dbg_dump("x1", x1[:, 0:4, :], b_x1[0:4])
                S.barrier()
            S.barrier()
            A.release(m_b2)

            if stop <= 5:
                raise _Stop()
            mT = mixedT.rearrange("p a b -> p (a b)").rearrange("p (a b) -> p a b", a=32)
            h2T = A.alloc([128, 8, 512], BF16)
            wupt = A.alloc([128, 2, 8, 256], BF16)
            sq2 = A.alloc([128, 1, 512], F32)
            b_mT = S.bufs(32, "mT")
            b_h2T = S.bufs(4, "h2T")
            b_wupt = S.bufs(2, "wupt")
            b_sq2 = S.bufs(1, "sq2") * 2
            b_yn2 = S.bufs(2, "yn2")
            b_out = S.bufs(NT, "out")
            yn2 = xs
            dma("sp", gb[:, 0, :], gvec[2:3, :].partition_broadcast(128), b_gb[0], [], [b_gb[0]])
            dma("sp", gb[:, 1, :], gvec[3:4, :].partition_broadcast(128), b_gb[1], [], [b_gb[1]])
            dma("pool", wdown[:, 24:32, :], wdown_d[:, 24:32, :], b_wdown[3], [], [b_wdown[3]])
            dma("pool", wupt[:, 0], wup_d[0], b_wupt[0], [], [b_wupt[0]])
            ucnt = 0
            for n in range(NG):
                for tl in range(4):
                    t = 4 * n + tl
                    i = t % 2
                    sc = stats[:, 104 + t:105 + t]
                    bs = sbuf_stat(104 + t)
                    act(junk, x1[:, t, :], AF.Square, [b_x1[t]], [bs], accum=sc)
                    rstd_from(sc, sc, D, [bs], bs)
                    stt("dve", hb[:, i, :], x1[:, t, :], sc, gb[:, 0, :], ALU.mult, ALU.mult, [b_x1[t], bs, b_gb[0]], [b_hb[i]])
                    pT = bankT[6 + i].rearrange("p (a b) -> p a b", a=8)
                    for c in range(8):
                        tr(pT[:, c, :], hb[:, i, c * 128:(c + 1) * 128], [b_hb[i]], [pb[6 + i]], c == 7)
                    cp("act", h2T[:, :, tl * 128:(tl + 1) * 128], pT, [pb[6 + i]], [b_h2T[tl]])
                for ch in range(16):
                    ws = ucnt % 2
                    nxt = ucnt + 1
                    if nxt < 16 * NG:
                        dma("pool", wupt[:, nxt % 2], wup_d[nxt % 16], b_wupt[nxt % 2], [], [b_wupt[nxt % 2]])
                    for sub in range(2):
                        fc = ch * 2 + sub
                        bi = fc % 2
                        for kc in range(8):
                            mm(banks[bi], wupt[:, ws, kc, sub * 128:(sub + 1) * 128], h2T[:, kc, :], kc == 0, kc == 7,
                               [b_wupt[ws]] + b_h2T, [pb[bi]], kc == 7)
                        act(sq2[:, 0, :], banks[bi], AF.Square, [pb[bi]], [b_sq2[bi]])
                        stt("dve", mT[:, fc, :], banks[bi], 0.0, sq2[:, 0, :], ALU.is_gt, ALU.mult,
                            [pb[bi], b_sq2[bi]], [b_mT[fc]])
                    ucnt += 1
                for tl in range(4):
                    t = 4 * n + tl
                    i = t % 2
                    b0 = 2 + 2 * i
                    for half in range(2):
                        for fc in range(32):
                            mm(banks[b0 + half], mT[:, fc, tl * 128:(tl + 1) * 128], wdown[:, fc, half * 512:(half + 1) * 512],
                               fc == 0, fc == 31, [b_mT[fc], b_wdown[fc // 8]], [pb[b0 + half]], fc == 31)
                    sc2 = stats[:, 128 + 2 * t:130 + 2 * t]
                    bs = sbuf_stat(128 + t)
                    for half in range(2):
                        act(junk[:, 0:512], banks[b0 + half], AF.Square, [pb[b0 + half]], [bs], accum=sc2[:, half:half + 1])
                    tt("dve", sc2[:, 0:1], sc2[:, 0:1], sc2[:, 1:2], ALU.add, [bs], [bs])
                    rstd_from(sc2[:, 0:1], sc2[:, 0:1], D, [bs], bs)
                    for half in range(2):
                        hs = slice(half * 512, (half + 1) * 512)
                        stt("dve", yn2[:, i, hs], banks[b0 + half], sc2[:, 0:1], gb[:, 1, hs], ALU.mult, ALU.mult,
                            [pb[b0 + half], bs, b_gb[1]], [b_yn2[i]])
                    tt("pool", x1[:, t, :], yn2[:, i, :], x1[:, t, :], ALU.add, [b_yn2[i], b_x1[t]], [b_x1[t]])
                    dma("sp", y[t * 128:(t + 1) * 128, :], x1[:, t, :], b_out[t], [b_x1[t]], [b_out[t]])
        except _Stop:
            pass
        S.final_wait("sp")
        S.emit()
        build_nc.peak = A.peak
    return nc


def _pk(w, kc):
    n = w.shape[1]
    return np.ascontiguousarray(w.reshape(kc, 128, n).transpose(1, 0, 2))


def _swap_halves(w, blk):
    r, c = w.shape
    v = w.reshape(r, c // blk, 2, blk // 2)
    return np.ascontiguousarray(v[:, :, ::-1, :]).reshape(r, c)


def _rope_tables(d, npos):
    inv = 1.0 / (10000.0 ** (np.arange(0, d, 2, dtype=np.float32) / d))
    ang = np.arange(npos, dtype=np.float32)[:, None] * inv[None, :]
    cos = np.cos(ang).astype(np.float32).T
    sin = np.sin(ang).astype(np.float32).T
    cosf = np.concatenate([cos, cos], 0)
    sinf = np.concatenate([-sin, sin], 0)
    return cosf, sinf


def _prep_shared(inp):
    f = lambda k: np.asarray(inp[k], dtype=np.float32)
    w_in = f("w_in")[0]
    qa_w = w_in[:, 0:256]
    ckv_w = w_in[:, 256:384]
    kr_w = w_in[:, 384:416]
    dq_w = w_in[:, 416:1440]
    dk_w = w_in[:, 1440:2464]
    dv_w = w_in[:, 2464:3488]
    g_w = w_in[:, 3488:5536]
    sh = {}
    sh["gvec"] = np.concatenate([f("pre_attn_g"), f("post_attn_g"), f("pre_mlp_g"), f("post_mlp_g")], 0)
    sh["gq"] = np.ascontiguousarray(f("q_norm_g")[0].reshape(2, 128).T)
    sh["gkv"] = np.ascontiguousarray(f("kv_norm_g")[0].reshape(128, 1))
    sh["gsub"] = f("subln_g").reshape(1, 128)
    sh["lamv"] = np.concatenate([f("lambda_q1"), f("lambda_k1"), f("lambda_q2"), f("lambda_k2")], 1)
    sh["ident"] = np.eye(128, dtype=np.float32)
    sh["tri"] = np.triu(np.ones((128, 128), dtype=np.float32))
    c64, s64 = _rope_tables(64, S_LEN)
    csd = np.zeros((128, 2, S_LEN), np.float32)
    csd[0:64, 0], csd[64:128, 0] = c64, c64
    csd[0:64, 1], csd[64:128, 1] = s64, s64
    sh["csd"] = csd
    c32, s32 = _rope_tables(32, S_LEN)
    csm = np.zeros((128, 2, S_LEN), np.float32)
    csm[0:32, 0], csm[0:32, 1] = c32, s32
    csm[64:96, 0], csm[64:96, 1] = c32, s32
    sh["csm"] = csm
    sh["wm"] = _pk(np.concatenate([qa_w, ckv_w, kr_w, _swap_halves(kr_w, 32)], 1), 8)
    wqb = f("w_q_b")[0]
    wqb_sw = wqb.reshape(256, 8, 96).copy()
    wqb_sw[:, :, 64:96] = _swap_halves(wqb.reshape(256, 8, 96)[:, :, 64:96].reshape(256, 256), 32).reshape(256, 8, 32)
    sh["wqb"] = _pk(np.concatenate([wqb, wqb_sw.reshape(256, 768)], 1), 2)
    wkvb = f("w_kv_b")[0].reshape(128, 8, 128)
    sh["wkvb"] = np.ascontiguousarray(np.concatenate([wkvb[:, :, 0:64].reshape(128, 512),
                                                      wkvb[:, :, 64:128].reshape(128, 512)], 1))
    dq_sw = _swap_halves(dq_w, 64)
    dk_sw = _swap_halves(dk_w, 64)
    wd = np.stack([np.concatenate([dq_w[:, h * 128:(h + 1) * 128], dq_sw[:, h * 128:(h + 1) * 128],
                                   dk_w[:, h * 128:(h + 1) * 128], dk_sw[:, h * 128:(h + 1) * 128]], 1)
                   for h in range(8)], 0)
    sh["wd"] = np.ascontiguousarray(wd.reshape(8, 8, 128, 512).transpose(0, 2, 1, 3))
    sh["wdv"] = _pk(dv_w, 8)
    wbm = f("w_br_mla")[0]
    wbd = f("w_br_diff")[0]
    wb = []
    for c in range(8):
        cs = slice(c * 128, (c + 1) * 128)
        blk = np.concatenate([wbm[:, cs].reshape(4, 128, 128), wbd[:, cs].reshape(8, 128, 128),
                              g_w[:, cs].reshape(8, 128, 128), g_w[:, 1024 + c * 128:1024 + (c + 1) * 128].reshape(8, 128, 128)], 0)
        wb.append(blk.transpose(1, 0, 2))
    sh["wb"] = np.ascontiguousarray(np.stack(wb, 0))
    sh["wout"] = _pk(f("w_out")[0], 8)
    wup = f("w_mlp_up")[0]
    sh["wup"] = np.ascontiguousarray(wup.reshape(8, 128, 16, 256).transpose(2, 1, 0, 3))
    sh["wdown"] = _pk(f("w_mlp_down")[0], 32)
    return sh


_NC_CACHE = {}


def kernel(**inputs):
    sh = _prep_shared(inputs)
    xfull = np.asarray(inputs["x"], dtype=np.float32)
    if "nc" not in _NC_CACHE:
        _NC_CACHE["nc"] = build_nc()
    nc = _NC_CACHE["nc"]
    in_maps = []
    for c in range(8):
        m = dict(sh)
        m["x"] = np.ascontiguousarray(xfull[c])
        in_maps.append(m)
    res = run_bass_kernel_spmd(nc, in_maps, core_ids=list(range(8)))
    out = np.stack([np.asarray(r["y"], dtype=np.float32) for r in res.results], 0)
    return out
```
